# Optimizing a Trainium2 kernel written in Bass

```python
import jax, jax.numpy as jnp
from jax import lax
import numpy as np

D_MODEL = 1024
BATCH = 4
SEQ = 4096
DEPTH = 1

N_MEM = 256
CONV_WIDTH = D_MODEL
CONV_K = 3
POOL_WINDOWS = (2, 4, 8, 16)
POOL_GROUPS = len(POOL_WINDOWS)
POOL_WIDTH = D_MODEL
POOL_GROUP_DIM = POOL_WIDTH // POOL_GROUPS
X_HEADS = 4
X_HEAD_DIM = D_MODEL // X_HEADS
X_WIDTH = X_HEADS * X_HEAD_DIM
N_BRANCH = 3
D_FF = ((8 * D_MODEL // 3 + 255) // 256) * 256
EPS = 1e-6
IN_SPLITS = (CONV_WIDTH, CONV_WIDTH, CONV_WIDTH, POOL_WIDTH, X_WIDTH, D_MODEL, D_MODEL, D_MODEL)
D_IN = sum(IN_SPLITS)

kernel_name = "hybrid_gated_conv_pool_memxattn_block"


def rms_norm(x, g):
    xf = x.astype(jnp.float32)
    y = xf * lax.rsqrt(jnp.mean(xf * xf, axis=-1, keepdims=True) + EPS)
    return (y * g.astype(jnp.float32)).astype(x.dtype)


def causal_depthwise_conv(u, w):
    c = u.shape[-1]
    return lax.conv_general_dilated(
        u, w[:, None, :].astype(u.dtype), window_strides=(1,), padding=[(CONV_K - 1, 0)],
        dimension_numbers=("NWC", "WIO", "NWC"), feature_group_count=c)


def multiscale_causal_pool(u):
    b, s, _ = u.shape
    uf = u.astype(jnp.float32).reshape(b, s, POOL_GROUPS, POOL_GROUP_DIM)
    csum = jnp.cumsum(uf, axis=1)
    pos = jnp.arange(1, s + 1, dtype=jnp.int32)
    outs = []
    for g, w in enumerate(POOL_WINDOWS):
        cg = csum[:, :, g]
        lower = jnp.pad(cg, ((0, 0), (w, 0), (0, 0)))[:, :s]
        cnt = jnp.minimum(pos, w).astype(jnp.float32)[None, :, None]
        outs.append((cg - lower) / cnt - uf[:, :, g])
    return jnp.stack(outs, axis=2).astype(u.dtype)


def memory_cross_attention(q, mem_n, w_kv):
    b, s, _ = q.shape
    m = mem_n.shape[1]
    qh = q.reshape(b, s, X_HEADS, X_HEAD_DIM)
    kv = mem_n @ w_kv
    k, v = jnp.split(kv, 2, axis=-1)
    k = k.reshape(b, m, X_HEADS, X_HEAD_DIM)
    v = v.reshape(b, m, X_HEADS, X_HEAD_DIM)
    scores = jnp.einsum("bshd,bmhd->bhsm", qh, k).astype(jnp.float32) * (X_HEAD_DIM ** -0.5)
    probs = jax.nn.softmax(scores, axis=-1).astype(v.dtype)
    o = jnp.einsum("bhsm,bmhd->bshd", probs, v)
    return o.reshape(b, s, X_WIDTH)


def setup_inputs(seed: int = 0) -> dict:
    key = jax.random.key(seed)
    ks = jax.random.split(key, 20)
    f32 = jnp.float32
    L = DEPTH

    def nrm(k, shape, fan_in):
        return jax.random.normal(k, shape, f32) * (fan_in ** -0.5)

    def gain(k, shape):
        return 1.0 + 0.05 * jax.random.normal(k, shape, f32)

    return {
        "x": jax.random.normal(ks[0], (BATCH, SEQ, D_MODEL), f32),
        "mem": jax.random.normal(ks[1], (BATCH, N_MEM, D_MODEL), f32),
        "norm_mix": gain(ks[2], (L, D_MODEL)),
        "w_in": nrm(ks[3], (L, D_MODEL, D_IN), D_MODEL),
        "conv_w": nrm(ks[4], (L, CONV_K, CONV_WIDTH), CONV_K),
        "w_conv_out": nrm(ks[5], (L, CONV_WIDTH, D_MODEL), CONV_WIDTH),
        "w_pool": nrm(ks[6], (L, POOL_GROUPS, POOL_GROUP_DIM, POOL_GROUP_DIM), POOL_GROUP_DIM),
        "pool_scale": gain(ks[7], (L, POOL_WIDTH)),
        "norm_mem": gain(ks[8], (L, D_MODEL)),
        "w_kv": nrm(ks[9], (L, D_MODEL, 2 * X_WIDTH), D_MODEL),
        "w_xattn_out": nrm(ks[10], (L, X_WIDTH, D_MODEL), X_WIDTH),
        "w_out": nrm(ks[11], (L, D_MODEL, D_MODEL), D_MODEL),
        "norm_ffn": gain(ks[12], (L, D_MODEL)),
        "w_gate": nrm(ks[13], (L, D_MODEL, D_FF), D_MODEL),
        "w_up": nrm(ks[14], (L, D_MODEL, D_FF), D_MODEL),
        "w_down": nrm(ks[15], (L, D_FF, D_MODEL), D_FF),
        "norm_final": gain(ks[16], (D_MODEL,)),
    }


def reference(x, mem, norm_mix, w_in, conv_w, w_conv_out, w_pool, pool_scale, norm_mem,
              w_kv, w_xattn_out, w_out, norm_ffn, w_gate, w_up, w_down, norm_final):
    b, s, _ = x.shape
    offsets = np.cumsum((0,) + IN_SPLITS)
    for l in range(DEPTH):
        h = rms_norm(x, norm_mix[l])
        proj = h @ w_in[l]
        b_a, c_a, u_a, u_p, q_x, g_a, g_p, g_x = [
            proj[..., int(offsets[i]):int(offsets[i + 1])] for i in range(len(IN_SPLITS))]

        y_a = (b_a * causal_depthwise_conv(c_a * u_a, conv_w[l])) @ w_conv_out[l]

        pooled = multiscale_causal_pool(u_p)
        y_p = jnp.einsum("bsgc,gcd->bsgd", pooled, w_pool[l]).reshape(b, s, POOL_WIDTH) * pool_scale[l]

        mem_n = rms_norm(mem, norm_mem[l])
        y_x = memory_cross_attention(q_x, mem_n, w_kv[l]) @ w_xattn_out[l]

        merged = (jax.nn.sigmoid(g_a) * y_a + jax.nn.sigmoid(g_p) * y_p
                  + jax.nn.sigmoid(g_x) * y_x)
        x = x + merged @ w_out[l]

        h = rms_norm(x, norm_ffn[l])
        x = x + (jax.nn.silu(h @ w_gate[l]) * (h @ w_up[l])) @ w_down[l]
    return rms_norm(x, norm_final)
```

```python
import numpy as np
import concourse.bass as bass
import concourse.mybir as mybir
from concourse.bass_utils import run_bass_kernel_spmd

F32 = mybir.dt.float32
BF16 = mybir.dt.bfloat16
AF = mybir.ActivationFunctionType
ALU = mybir.AluOpType

P = 128
D = 1024
KC = 8
TM = 1024
HALO = 16
TH = TM + HALO
NT = TM // P
DIN = 8192
DFF = 2816
FH = 11
NMEM = 256
SLOTW = 256
RING = 12
POOL_ADD_ENGINE = "dve"
EPS = 1e-6
N_CORES = 8


class Plan:
    ENG = ("pe", "act", "dve", "pool", "sp")
    ND = 8

    def __init__(self):
        self.ops = {e: [] for e in self.ENG}
        self.cnt = {e: 0 for e in self.ENG}
        self.seen = {e: {} for e in self.ENG}
        self.last_w = {}
        self.readers = {}
        self.dma_n = {"sp": 0, "pool": 0}
        self.dma_uses = {}

    def op(self, eng, fn, reads=(), writes=(), dma=False):
        deps = set()
        for k in reads:
            if k in self.last_w:
                deps.add(self.last_w[k])
        for k in writes:
            if k in self.last_w:
                deps.add(self.last_w[k])
            deps.update(self.readers.get(k, ()))
        if dma:
            i = self.dma_n[eng] % self.ND
            self.dma_n[eng] += 1
            semkey = ("dma", eng, i)
            uses = self.dma_uses.get(semkey, 0)
            if uses > 0:
                deps.add((semkey, 16 * uses))
            self.dma_uses[semkey] = uses + 1
            token = (semkey, 16 * (uses + 1))
        else:
            semkey = eng
            self.cnt[eng] += 1
            token = (eng, self.cnt[eng])
        best = {}
        for sk, v in deps:
            if v > best.get(sk, 0):
                best[sk] = v
        waits = []
        for sk, v in best.items():
            if sk == "pe" and eng == "pe":
                continue
            if self.seen[eng].get(sk, 0) >= v:
                continue
            self.seen[eng][sk] = v
            waits.append((sk, v))
        self.ops[eng].append((waits, fn, semkey, 16 if dma else 1))
        for k in reads:
            self.readers.setdefault(k, []).append(token)
        for k in writes:
            self.last_w[k] = token
            self.readers[k] = []
        return token


def build_program(debug=False):
    nc = bass.Bass("TRN2", target_bir_lowering=False)

    def din(name, shape):
        return nc.dram_tensor(name, list(shape), F32, kind="ExternalInput")

    x_d = din("x", (2, TH, D))
    mem_d = din("mem", (NMEM, D))
    gains_d = din("gains", (4, P, D))
    small_d = din("small", (P, 160))
    ident_d = din("ident", (P, P))
    w_in_d = din("w_in", (D, DIN))
    w_co_d = din("w_conv_out", (D, D))
    w_pool_d = din("w_pool", (4, 256, 256))
    w_kv_d = din("w_kv", (D, 2 * D))
    w_xo_d = din("w_xattn_out", (D, D))
    w_out_d = din("w_out", (D, D))
    w_gate_d = din("w_gate", (D, DFF))
    w_up_d = din("w_up", (D, DFF))
    w_down_d = din("w_down", (DFF, D))
    out_d = nc.dram_tensor("out", [2 * TM, D], F32, kind="ExternalOutput")
    dbg = {}

    def sb(name, shape, dt):
        return nc.alloc_sbuf_tensor(name, list(shape), dt)

    ring = sb("ring", (P, RING, KC, SLOTW), BF16)
    NH = KC * TH
    NA = 38 * 1024
    AR = sb("arena", (P, NH + NA), BF16)
    hT = AR[:, 0:NH].rearrange("p (k t) -> p k t", k=KC)

    def atoms(a0, n):
        return AR[:, NH + a0 * 1024: NH + (a0 + n) * 1024]

    zT = atoms(0, 8).rearrange("p (k t) -> p k t", k=8)
    plT = atoms(8, 8).rearrange("p (k t) -> p k t", k=8)
    oT = atoms(16, 8).rearrange("p (k t) -> p k t", k=8)
    qT = atoms(24, 8).rearrange("p (k t) -> p k t", k=8)
    mT = qT
    x1 = atoms(0, 16).bitcast(F32).rearrange("p (i f) -> p i f", i=NT)
    actT = atoms(16, FH).rearrange("p (k t) -> p k t", k=FH)
    wdT = atoms(27, FH).rearrange("p (k t) -> p k t", k=FH)

    def kA(a0, n=1):
        return [("A", a) for a in range(a0, a0 + n)]

    kH = [("Ht", t) for t in range(1, NT + 1)]

    def kHt(c0):
        return [("Ht", 0 if c0 == 0 else 1 + (c0 - HALO) // P)]

    T = [sb("tmp%d" % i, (P, TH), F32) for i in range(6)]
    gBs = [sb("gB%d" % i, (P, D), F32) for i in range(2)]
    hb = [sb("hb%d" % i, (P, D), BF16) for i in range(4)]
    junk = sb("junk", (P, D), BF16)
    kT = sb("kT", (P, KC, NMEM), BF16)
    vv = sb("vv", (P, 2, D), BF16)
    memT = T[5][:, 0:1024].bitcast(BF16).rearrange("p (k t) -> p k t", k=KC)
    eT = [sb("eT%d" % i, (P, 1024), BF16) for i in range(2)]
    rden = [sb("rden%d" % i, (P, 512), F32) for i in range(2)]
    wpool = sb("wpool", (P, 4, 2, 256), BF16)
    ident = sb("identb", (P, P), BF16)
    ones = sb("onesb", (P, P), BF16)
    small = sb("smallc", (P, 160), F32)
    stat = sb("stat", (P, 96), F32)
    tmp16 = sb("tmp16", (P, 16), F32)
    pst = sb("pstash", (P, KC, HALO), F32)
    ust = sb("ustash", (P, KC, HALO), F32)

    PS = nc.alloc_psum_tensor("ps", [P, 4096], F32)
    NPU = 3
    PU = [PS[:, u * 1024:(u + 1) * 1024] for u in range(NPU)]
    HS = PS[:, 3072:3584]
    DEN = PS[:, 3584:4096]
    PTv = [PU[u].bitcast(BF16)[:, 0:1024].rearrange("p (k t) -> p k t", k=8) for u in range(NPU)]

    plan = Plan()
    st = {"slot": 0, "pu": 0, "hs": 0, "stat": 0, "hb": 0}

    def dump(name, ap, keys, shape, dt):
        if not debug:
            return
        d = nc.dram_tensor("dbg_" + name, list(shape), dt, kind="ExternalOutput")
        dbg[name] = d
        plan.op("sp", lambda e, d=d, ap=ap: e.dma_start(out=d.ap(), in_=ap), reads=keys,
                writes=[("dbg", name)], dma=True)

    def slot_load(W, r0, nk, c0, ncols=SLOTW, after=()):
        s = st["slot"] % RING
        st["slot"] += 1
        src = W[r0:r0 + nk * P, c0:c0 + ncols].rearrange("(k p) n -> p k n", p=P)
        dst = ring[:, s, 0:nk, 0:ncols]
        plan.op("pool", lambda e: e.dma_start(out=dst, in_=src), reads=list(after), writes=[("S", s)], dma=True)
        return s

    def pe_unit(mms, reads, u=None, writes=None):
        def fn(e):
            ins = None
            for (o, l, r, s0, s1) in mms:
                ins = e.matmul(o, l, r, start=s0, stop=s1)
            return ins
        w = writes if writes is not None else [("PU", u)]
        plan.op("pe", fn, reads=reads, writes=w)

    held = set()

    def next_pu():
        while True:
            u = st["pu"] % NPU
            st["pu"] += 1
            if u not in held:
                return u

    def proj_unit(slot, colofs, halo, src=None, srckeys=None, split=False):
        own = src is None
        src = hT if src is None else src
        u = next_pu()
        hp = None
        if halo:
            hp = st["hs"] % 32
            st["hs"] += 1
        blk = [[], []]
        for kc in range(KC):
            lw = ring[:, slot, kc, colofs:colofs + P]
            blk[0].append((PU[u][:, 0:512], lw, src[:, kc, 16:528], kc == 0, kc == KC - 1))
            blk[1].append((PU[u][:, 512:1024], lw, src[:, kc, 528:1040], kc == 0, kc == KC - 1))
            if halo:
                blk[1].append((HS[:, hp * 16:(hp + 1) * 16], lw, src[:, kc, 0:16], kc == 0, kc == KC - 1))
        wr = [("PU", u)] + ([("HS",)] if halo else [])
        if own:
            k0, k1 = kH[0:4], kH[4:8] + ([("Ht", 0)] if halo else [])
        else:
            k0 = k1 = srckeys
        if split:
            pe_unit(blk[0], [("S", slot)] + k0, writes=[("PU", u)])
            pe_unit(blk[1], [("S", slot)] + k1, writes=wr)
        else:
            mms = []
            n1 = len(blk[1]) // KC
            for kc in range(KC):
                mms.append(blk[0][kc])
                mms.extend(blk[1][kc * n1:(kc + 1) * n1])
            pe_unit(mms, [("S", slot)] + list(k0) + [k for k in k1 if k not in k0], writes=wr)
        return (u, hp) if halo else u

    def proj_begin(slot, colofs):
        u = next_pu()
        mms = [(PU[u][:, 0:512], ring[:, slot, kc, colofs:colofs + P], hT[:, kc, 16:528], kc == 0, kc == KC - 1)
               for kc in range(KC)]
        pe_unit(mms, [("S", slot)] + kH[0:4], writes=[("PU", u)])
        held.add(u)
        return (u, slot, colofs)

    def proj_end(h):
        u, slot, colofs = h
        mms = [(PU[u][:, 512:1024], ring[:, slot, kc, colofs:colofs + P], hT[:, kc, 528:1040], kc == 0, kc == KC - 1)
               for kc in range(KC)]
        pe_unit(mms, [("S", slot)] + kH[4:8], writes=[("PU", u)])
        held.discard(u)
        return u

    def act(fn, reads, writes):
        plan.op("act", fn, reads=reads, writes=writes)

    def dve(fn, reads, writes):
        plan.op("dve", fn, reads=reads, writes=writes)

    def gps(fn, reads, writes):
        plan.op(POOL_ADD_ENGINE, fn, reads=reads, writes=writes)

    def new_stat():
        c = st["stat"] % 32
        st["stat"] += 1
        return c

    def rms_tile(src_ap, src_keys, rows, dst_ap, dst_keys, gi):
        c = new_stat()
        gB = gBs[gi]
        ss, sd, rs = stat[:rows, c:c + 1], stat[:rows, 32 + c:33 + c], stat[:rows, 64 + c:65 + c]
        act(lambda e: e.activation(out=junk[:rows, :], in_=src_ap, func=AF.Square, accum_out=ss),
            reads=src_keys, writes=[("junk",), ("st", c)])
        act(lambda e: e.activation(out=sd, in_=ss, func=AF.Sqrt, bias=EPS, scale=1.0 / D),
            reads=[("st", c)], writes=[("st", 32 + c)])
        dve(lambda e: e.reciprocal(out=rs, in_=sd), reads=[("st", 32 + c)], writes=[("st", 64 + c)])
        dve(lambda e: e.scalar_tensor_tensor(out=dst_ap, in0=src_ap, scalar=rs, in1=gB[:rows, :],
                                             op0=ALU.mult, op1=ALU.mult),
            reads=src_keys + [("st", 64 + c), ("gB", gi)], writes=dst_keys)

    def transpose_tile(src_bf, src_keys, rows, dstT, c0, dst_keys, on_dve=False):
        m = next_pu()

        def fn(e):
            ins = None
            for kc in range(KC):
                ins = e.transpose(PTv[m][:, kc, 0:rows], src_bf[:rows, kc * P:(kc + 1) * P], ident[:rows, :rows])
            return ins
        plan.op("pe", fn, reads=src_keys + [("ident",)], writes=[("PU", m)])
        if on_dve:
            dve(lambda e: e.tensor_copy(out=dstT[:, :, c0:c0 + rows], in_=PTv[m][:, :, 0:rows]),
                reads=[("PU", m)], writes=dst_keys)
        else:
            act(lambda e: e.activation(out=dstT[:, :, c0:c0 + rows], in_=PTv[m][:, :, 0:rows], func=AF.Copy),
                reads=[("PU", m)], writes=dst_keys)

    def load_gain(gi, which):
        plan.op("sp", lambda e: e.dma_start(out=gBs[gi][:, :], in_=gains_d[which]), writes=[("gB", gi)], dma=True)

    class Stream:
        def __init__(self, tiles, skew, lead=2):
            self.tiles, self.skew, self.lead = tiles, skew, lead
            self.i, self.l, self.pending = 0, 0, []

        def _loads(self, upto):
            while self.l < min(upto, len(self.tiles)):
                s0 = self.tiles[self.l][0]
                if s0 is not None:
                    s0()
                self.l += 1

        def advance(self):
            self._loads(self.i + 1 + self.lead)
            if self.i < len(self.tiles):
                _, s1, s2 = self.tiles[self.i]
                self.i += 1
                self.pending.append((s2, s1()))
            if len(self.pending) > self.skew or (self.i >= len(self.tiles) and self.pending):
                s2, b = self.pending.pop(0)
                s2(b)

        def finish(self):
            while self.i < len(self.tiles) or self.pending:
                self.advance()

        def drain_keep_last(self):
            self._loads(len(self.tiles))
            while self.i < len(self.tiles):
                _, s1, s2 = self.tiles[self.i]
                self.i += 1
                self.pending.append((s2, s1()))
            while len(self.pending) > 1:
                s2, b = self.pending.pop(0)
                s2(b)

    def next_hb():
        b = st["hb"] % 4
        st["hb"] += 1
        return b

    plan.op("sp", lambda e: e.dma_start(out=small[:, :], in_=small_d.ap()), writes=[("small",)], dma=True)
    plan.op("pool", lambda e: e.dma_start(out=ident[:, :], in_=ident_d.ap()), writes=[("ident",)], dma=True)
    plan.op("pool", lambda e: e.dma_start(out=wpool[:, :, :, :],
                                          in_=w_pool_d.ap().rearrange("g (k p) n -> p g k n", p=P)),
            writes=[("wpool",)], dma=True)
    dve(lambda e: e.memset(ones[:, :], 1.0), reads=[], writes=[("ones",)])
    dve(lambda e: e.memset(tmp16[:, :], 1.0), reads=[], writes=[("tmp16",)])
    act(lambda e: e.activation(out=tmp16[:, 0:1], in_=tmp16[:, 1:2], func=AF.Sqrt),
        reads=[("tmp16",)], writes=[("tmp16",)])

    def cw(j, k):
        return small[:, j * 3 + k: j * 3 + k + 1]

    def pscale(j):
        return small[:, 24 + j: 25 + j]

    def invc(hf, g):
        return small[:, 32 + hf * 64 + g * 16: 32 + hf * 64 + (g + 1) * 16]

    def _shift(t3):
        class _V:
            def __getitem__(self, key):
                p, k, sl = key
                return t3[p, k, sl.start - HALO: sl.stop - HALO]
        return _V()

    def make_A_stream(hf, skew, staged=False):
        tiles = []
        for i in range(1 if hf == 1 else 0, NT + 1):
            rows = HALO if i == 0 else P
            r0 = 0 if i == 0 else HALO + (i - 1) * P
            from_x1 = staged and i > 0

            def s0(i=i, rows=rows, r0=r0):
                xt = T[i % 3]
                plan.op("sp", lambda e: e.dma_start(out=xt[:rows, 0:D], in_=x_d[hf, r0:r0 + rows, :]),
                        writes=[("T", i % 3)], dma=True)

            def s1(i=i, rows=rows, r0=r0, from_x1=from_x1):
                b = next_hb()
                if from_x1:
                    rms_tile(x1[:, i - 1, :], kA(2 * (i - 1), 2), rows, hb[b][:rows, :], [("hb", b)], 0)
                else:
                    xt = T[i % 3]
                    rms_tile(xt[:rows, 0:D], [("T", i % 3)], rows, hb[b][:rows, :], [("hb", b)], 0)
                return b

            def s2(b, i=i, rows=rows, r0=r0):
                transpose_tile(hb[b], [("hb", b)], rows, hT, r0, kHt(r0), on_dve=(staged and i % 2 == 1))
            tiles.append((None if from_x1 else s0, s1, s2))
        return Stream(tiles, skew)

    def make_C_stream(skew):
        tiles = []
        for i in range(NT):
            def s1(i=i):
                b = next_hb()
                rms_tile(x1[:, i, :], kA(2 * i, 2), P, hb[b][:, :], [("hb", b)], 0)
                return b

            def s2(b, i=i):
                transpose_tile(hb[b], [("hb", b)], P, hT, HALO + i * P, kHt(HALO + i * P))
            tiles.append((None, s1, s2))
        return Stream(tiles, skew)

    def phase_K_loads():
        load_gain(1, 1)
        for mt in range(2):
            xt = T[3 + mt]
            plan.op("sp", lambda e, xt=xt, mt=mt: e.dma_start(out=xt[:, 0:D], in_=mem_d[mt * P:(mt + 1) * P, :]),
                    writes=[("T", 3 + mt)], dma=True)

    kn = {"hb": []}

    def phase_K_norm1():
        for mt in range(2):
            xt = T[3 + mt]
            b = next_hb()
            rms_tile(xt[:, 0:D], [("T", 3 + mt)], P, hb[b][:, :], [("hb", b)], 1)
            kn["hb"].append(b)

    def phase_K_norm2():
        for mt in range(2):
            b = kn["hb"][mt]
            transpose_tile(hb[b], [("hb", b)], P, memT, mt * P, [("T", 5)])

    def phase_K_units():
        for g in range(4):
            s = slot_load(w_kv_d, 0, KC, g * SLOTW)
            for jj in range(2):
                j = g * 2 + jj
                u = next_pu()
                mms = [(PU[u][:, 0:NMEM], ring[:, s, kc, jj * P:(jj + 1) * P], memT[:, kc, :], kc == 0, kc == KC - 1)
                       for kc in range(KC)]
                pe_unit(mms, [("S", s), ("T", 5)], u)
                act(lambda e, u=u, j=j: e.activation(out=kT[:, j, :], in_=PU[u][:, 0:NMEM], func=AF.Copy),
                    reads=[("PU", u)], writes=[("kT",)])
        for g in range(4):
            s = slot_load(w_kv_d, 0, KC, D + g * SLOTW)
            for mt in range(2):
                u = next_pu()
                mms = [(PU[u][:, 0:SLOTW], memT[:, kc, mt * P:(mt + 1) * P], ring[:, s, kc, :], kc == 0, kc == KC - 1)
                       for kc in range(KC)]
                pe_unit(mms, [("S", s), ("T", 5)], u)
                act(lambda e, u=u, mt=mt, g=g: e.activation(out=vv[:, mt, g * SLOTW:(g + 1) * SLOTW],
                                                            in_=PU[u][:, 0:SLOTW], func=AF.Copy),
                    reads=[("PU", u)], writes=[("vv",)])
        dump("kT", kT[:, :, :], [("kT",)], (P, KC, NMEM), BF16)
        dump("vv", vv[:, :, :], [("vv",)], (P, 2, D), BF16)

    def conv_chunk_a(hf, j, sC, jj):
        ca = T[0]
        if hf == 1:
            u = proj_unit(sC, jj * P, False)
            act(lambda e: e.activation(out=ca[:, HALO:TH], in_=PU[u][:, 0:TM], func=AF.Copy),
                reads=[("PU", u)], writes=[("T", 0)])
            return
        u, hp = proj_unit(sC, jj * P, True)
        act(lambda e: e.activation(out=ca[:, 0:HALO], in_=HS[:, hp * 16:(hp + 1) * 16], func=AF.Copy),
            reads=[("HS",)], writes=[("T", 0)])
        act(lambda e: e.activation(out=ca[:, HALO:TH], in_=PU[u][:, 0:TM], func=AF.Copy),
            reads=[("PU", u), ("T", 0)], writes=[("T", 0)])

    def conv_chunk_b(hf, j, sU, jj):
        ca, pp, cv = T[0], T[1], T[2]
        if hf == 1:
            u2 = proj_unit(sU, jj * P, False)
            dve(lambda e: e.tensor_copy(out=pp[:, 0:HALO], in_=pst[:, j, :]),
                reads=[("pst", j)], writes=[("T", 1)])
        else:
            u2, hp2 = proj_unit(sU, jj * P, True)
            dve(lambda e: e.tensor_tensor(out=pp[:, 0:HALO], in0=HS[:, hp2 * 16:(hp2 + 1) * 16], in1=ca[:, 0:HALO],
                                          op=ALU.mult),
                reads=[("HS",), ("T", 0)], writes=[("T", 1)])
        dve(lambda e: e.tensor_tensor(out=pp[:, HALO:TH], in0=PU[u2][:, 0:TM], in1=ca[:, HALO:TH], op=ALU.mult),
            reads=[("PU", u2), ("T", 0), ("T", 1)], writes=[("T", 1)])
        if hf == 0:
            dve(lambda e: e.tensor_copy(out=pst[:, j, :], in_=pp[:, TH - HALO:TH]),
                reads=[("T", 1)], writes=[("pst", j)])
        act(lambda e: e.mul(out=cv[:, 0:TM], in_=pp[:, 14:14 + TM], mul=cw(j, 0)),
            reads=[("T", 1), ("small",)], writes=[("T", 2)])
        dve(lambda e: e.scalar_tensor_tensor(out=cv[:, 0:TM], in0=pp[:, 15:15 + TM], scalar=cw(j, 1),
                                             in1=cv[:, 0:TM], op0=ALU.mult, op1=ALU.add),
            reads=[("T", 1), ("T", 2), ("small",)], writes=[("T", 2)])
        dve(lambda e: e.scalar_tensor_tensor(out=cv[:, 0:TM], in0=pp[:, 16:16 + TM], scalar=cw(j, 2),
                                             in1=cv[:, 0:TM], op0=ALU.mult, op1=ALU.add),
            reads=[("T", 1), ("T", 2), ("small",)], writes=[("T", 2)])

    def conv_chunk2(j, sB, jj):
        cv = T[2]
        u3 = proj_unit(sB, jj * P, False)
        dve(lambda e: e.tensor_tensor(out=zT[:, j, :], in0=PU[u3][:, 0:TM], in1=cv[:, 0:TM], op=ALU.mult),
            reads=[("PU", u3), ("T", 2)], writes=kA(j))

    def pool_chunk(hf, j, sP, jj):
        up, sA, sBt = T[3], T[4], T[5]
        grp = j // 2
        w = 2 << grp
        if hf == 1:
            u = proj_unit(sP, jj * P, False)
            act(lambda e: e.activation(out=up[:, 0:HALO], in_=ust[:, j, :], func=AF.Copy),
                reads=[("ust", j)], writes=[("T", 3)])
        else:
            u, hp = proj_unit(sP, jj * P, True)
            act(lambda e: e.activation(out=up[:, 0:HALO], in_=HS[:, hp * 16:(hp + 1) * 16], func=AF.Copy),
                reads=[("HS",)], writes=[("T", 3)])
        act(lambda e: e.activation(out=up[:, HALO:TH], in_=PU[u][:, 0:TM], func=AF.Copy),
            reads=[("PU", u), ("T", 3)], writes=[("T", 3)])
        if hf == 0:
            act(lambda e: e.activation(out=ust[:, j, :], in_=up[:, TH - HALO:TH], func=AF.Copy),
                reads=[("T", 3)], writes=[("ust", j)])
        gps(lambda e: e.tensor_tensor(out=sA[:, 1:TH], in0=up[:, 1:TH], in1=up[:, 0:TH - 1], op=ALU.add),
            reads=[("T", 3)], writes=[("T", 4)])
        S, Sk = sA, 4
        if w >= 4:
            gps(lambda e: e.tensor_tensor(out=sBt[:, 3:TH], in0=sA[:, 3:TH], in1=sA[:, 1:TH - 2], op=ALU.add),
                reads=[("T", 4)], writes=[("T", 5)])
            S, Sk = sBt, 5
        if w >= 8:
            gps(lambda e: e.tensor_tensor(out=sA[:, 7:TH], in0=sBt[:, 7:TH], in1=sBt[:, 3:TH - 4], op=ALU.add),
                reads=[("T", 5)], writes=[("T", 4)])
            S, Sk = sA, 4
        if w >= 16:
            gps(lambda e: e.tensor_tensor(out=sBt[:, 15:TH], in0=sA[:, 15:TH], in1=sA[:, 7:TH - 8], op=ALU.add),
                reads=[("T", 4)], writes=[("T", 5)])
            S, Sk = sBt, 5
        dve(lambda e: e.scalar_tensor_tensor(out=plT[:, j, :], in0=S[:, HALO:TH], scalar=1.0 / w,
                                             in1=up[:, HALO:TH], op0=ALU.mult, op1=ALU.subtract),
            reads=[("T", Sk), ("T", 3)], writes=kA(8 + j))
        dve(lambda e: e.tensor_tensor(out=tmp16[:, :], in0=S[:, HALO:2 * HALO], in1=invc(hf, grp), op=ALU.mult),
            reads=[("T", Sk), ("small",)], writes=[("tmp16",)])
        dve(lambda e: e.tensor_tensor(out=plT[:, j, 0:HALO], in0=tmp16[:, :], in1=up[:, HALO:2 * HALO],
                                      op=ALU.subtract),
            reads=[("tmp16",), ("T", 3)], writes=kA(8 + j))

    def q_chunk(j, sQ, jj):
        u = proj_unit(sQ, jj * P, False, split=(j < 2))
        act(lambda e, u=u: e.activation(out=qT[:, j, :], in_=PU[u][:, 0:TM], func=AF.Copy),
            reads=[("PU", u)], writes=kA(24 + j))

    def attn_S(it):
        nb, h = it % 2, it // 2
        b = it % 2
        u = next_pu()
        mms = []
        for mc in range(2):
            for dc in range(2):
                mms.append((PU[u][:, mc * 512:(mc + 1) * 512], kT[:, 2 * h + dc, mc * P:(mc + 1) * P],
                            qT[:, 2 * h + dc, nb * 512:(nb + 1) * 512], dc == 0, dc == 1))
        pe_unit(mms, [("kT",)] + kA(24 + 2 * h, 2), u)
        act(lambda e: e.activation(out=eT[b][:, :], in_=PU[u][:, 0:1024], func=AF.Exp, scale=1.0 / 16.0),
            reads=[("PU", u)], writes=[("eT", b)])

    def attn_O(it):
        nb, h = it % 2, it // 2
        b = it % 2
        u2 = next_pu()
        mms = []
        for dc in range(2):
            for mc in range(2):
                mms.append((PU[u2][:, dc * 512:(dc + 1) * 512], vv[:, mc, h * 256 + dc * P: h * 256 + (dc + 1) * P],
                            eT[b][:, mc * 512:(mc + 1) * 512], mc == 0, mc == 1))
        for mc in range(2):
            mms.append((DEN[:, :], ones[:, :], eT[b][:, mc * 512:(mc + 1) * 512], mc == 0, mc == 1))
        pe_unit(mms, [("vv",), ("ones",), ("eT", b)], writes=[("PU", u2), ("DEN",)])
        dve(lambda e: e.reciprocal(out=rden[b][:, :], in_=DEN[:, :]),
            reads=[("DEN",)], writes=[("rden", b)])
        for dc in range(2):
            dve(lambda e, dc=dc: e.tensor_tensor(
                out=oT[:, 2 * h + dc, nb * 512:(nb + 1) * 512], in0=PU[u2][:, dc * 512:(dc + 1) * 512],
                in1=rden[b][:, :], op=ALU.mult),
                reads=[("PU", u2), ("rden", b)], writes=kA(16 + 2 * h + dc))

    def merge_chunk(j, sGa, sCo, sGp, sGx, sXo, jj):
        sa, sp_, sx, mm_, t1, t2 = T[0], T[1], T[2], T[3], T[4], T[5]
        grp = j // 2
        u = proj_unit(sGa, jj * P, False)
        act(lambda e, u=u: e.activation(out=sa[:, 0:TM], in_=PU[u][:, 0:TM], func=AF.Sigmoid),
            reads=[("PU", u)], writes=[("T", 0)])
        u = proj_unit(sCo, jj * P, False, src=_shift(zT), srckeys=kA(0, 8))
        dve(lambda e, u=u: e.tensor_tensor(out=mm_[:, 0:TM], in0=PU[u][:, 0:TM], in1=sa[:, 0:TM], op=ALU.mult),
            reads=[("PU", u), ("T", 0)], writes=[("T", 3)])
        u = proj_unit(sGp, jj * P, False)
        act(lambda e, u=u: e.activation(out=sp_[:, 0:TM], in_=PU[u][:, 0:TM], func=AF.Sigmoid),
            reads=[("PU", u)], writes=[("T", 1)])
        u = next_pu()
        mms = []
        for k2 in range(2):
            for nbk in range(2):
                mms.append((PU[u][:, nbk * 512:(nbk + 1) * 512], wpool[:, grp, k2, (j % 2) * P:(j % 2 + 1) * P],
                            plT[:, 2 * grp + k2, nbk * 512:(nbk + 1) * 512], k2 == 0, k2 == 1))
        pe_unit(mms, [("wpool",)] + kA(8 + 2 * grp, 2), u)
        dve(lambda e, u=u: e.scalar_tensor_tensor(out=t1[:, 0:TM], in0=PU[u][:, 0:TM], scalar=pscale(j),
                                                  in1=sp_[:, 0:TM], op0=ALU.mult, op1=ALU.mult),
            reads=[("PU", u), ("T", 1), ("small",)], writes=[("T", 4)])
        dve(lambda e: e.tensor_tensor(out=mm_[:, 0:TM], in0=mm_[:, 0:TM], in1=t1[:, 0:TM], op=ALU.add),
            reads=[("T", 3), ("T", 4)], writes=[("T", 3)])
        u = proj_unit(sGx, jj * P, False)
        act(lambda e, u=u: e.activation(out=sx[:, 0:TM], in_=PU[u][:, 0:TM], func=AF.Sigmoid),
            reads=[("PU", u)], writes=[("T", 2)])
        u = proj_unit(sXo, jj * P, False, src=_shift(oT), srckeys=kA(16, 8))
        dve(lambda e, u=u: e.tensor_tensor(out=t2[:, 0:TM], in0=PU[u][:, 0:TM], in1=sx[:, 0:TM], op=ALU.mult),
            reads=[("PU", u), ("T", 2)], writes=[("T", 5)])
        dve(lambda e: e.tensor_tensor(out=mT[:, j, :], in0=mm_[:, 0:TM], in1=t2[:, 0:TM], op=ALU.add),
            reads=[("T", 3), ("T", 5)], writes=kA(24 + j))

    def phase_B(hf, instream=None, pre=None):
        for g in range(4):
            if g == 0 and pre is not None:
                sQ, hq = pre
            else:
                sQ = slot_load(w_in_d, 0, KC, 4 * D + g * SLOTW)
            if g == 0 and (instream is not None or pre is not None):
                if pre is None:
                    hq = [proj_begin(sQ, 0), proj_begin(sQ, P)]
                    instream.finish()
                for jj in range(2):
                    u = proj_end(hq[jj])
                    act(lambda e, u=u, jj=jj: e.activation(out=qT[:, jj, :], in_=PU[u][:, 0:TM], func=AF.Copy),
                        reads=[("PU", u)], writes=kA(24 + jj))
                continue
            for jj in range(2):
                q_chunk(g * 2 + jj, sQ, jj)
            if hf == 0 and g == 1:
                phase_K_norm2()
        dump("qT%d" % hf, atoms(24, 8), kA(24, 8), (P, 8192), BF16)
        def load_cp(g):
            return (slot_load(w_in_d, 0, KC, 1 * D + g * SLOTW), slot_load(w_in_d, 0, KC, 2 * D + g * SLOTW),
                    slot_load(w_in_d, 0, KC, 0 * D + g * SLOTW), slot_load(w_in_d, 0, KC, 3 * D + g * SLOTW))

        def load_mg(g):
            return (slot_load(w_in_d, 0, KC, 5 * D + g * SLOTW), slot_load(w_co_d, 0, KC, g * SLOTW),
                    slot_load(w_in_d, 0, KC, 6 * D + g * SLOTW), slot_load(w_in_d, 0, KC, 7 * D + g * SLOTW),
                    slot_load(w_xo_d, 0, KC, g * SLOTW))
        if hf == 0:
            phase_K_units()
        nxt = load_cp(0)
        mg0 = None
        for g in range(4):
            sC, sU, sB, sP = nxt
            if g < 3:
                nxt = load_cp(g + 1)
            else:
                mg0 = load_mg(0)
            for jj in range(2):
                j = g * 2 + jj
                conv_chunk_a(hf, j, sC, jj)
                attn_S(j)
                conv_chunk_b(hf, j, sU, jj)
                conv_chunk2(j, sB, jj)
                attn_O(j)
                pool_chunk(hf, j, sP, jj)
        dump("zT%d" % hf, atoms(0, 8), kA(0, 8), (P, 8192), BF16)
        dump("plT%d" % hf, atoms(8, 8), kA(8, 8), (P, 8192), BF16)
        dump("oT%d" % hf, atoms(16, 8), kA(16, 8), (P, 8192), BF16)
        for g in range(4):
            sGa, sCo, sGp, sGx, sXo = mg0 if g == 0 else load_mg(g)
            for jj in range(2):
                merge_chunk(g * 2 + jj, sGa, sCo, sGp, sGx, sXo, jj)
        dump("mT%d" % hf, atoms(24, 8), kA(24, 8), (P, 8192), BF16)

    def phase_B5_C(hf):
        load_gain(0, 2)
        for hx in range(2):
            plan.op("sp", lambda e, hx=hx: e.dma_start(
                out=x1[:, hx * 4:(hx + 1) * 4, :],
                in_=x_d[hf, HALO + hx * 512:HALO + (hx + 1) * 512, :].rearrange("(i p) f -> p i f", p=P)),
                writes=kA(hx * 8, 8), dma=True)
        assert st["slot"] % 2 == 0
        sO = [slot_load(w_out_d, 0, KC, g * SLOTW) for g in range(4)]
        cs = make_C_stream(2)
        for i in range(NT):
            u = next_pu()

            def grp(kcs):
                mms = []
                for g2 in range(2):
                    for kc in kcs:
                        mms.append((PU[u][:, g2 * 512:(g2 + 1) * 512], mT[:, kc, i * P:(i + 1) * P],
                                    ring[:, sO[2 * g2]:sO[2 * g2] + 2, kc, :], kc == 0, kc == KC - 1))
                return mms
            if i == 0:
                pe_unit(grp(range(KC - 1)), [("S", s) for s in sO] + kA(24, 7), u)
                pe_unit(grp([KC - 1]), [("S", s) for s in sO] + kA(31, 1), u)
            else:
                pe_unit(grp(range(KC)), [("S", s) for s in sO] + kA(24, 8), u)
            if i == NT - 1:
                cs.advance()
            dve(lambda e, u=u, i=i: e.tensor_tensor(out=x1[:, i, :], in0=PU[u][:, 0:D], in1=x1[:, i, :], op=ALU.add),
                reads=[("PU", u)] + kA(2 * i, 2), writes=kA(2 * i, 2))
            if 1 <= i < NT - 1:
                cs.advance()
        cs.drain_keep_last()
        return cs

    def final_tile(hf, i):
        ot = T[3 + (i % 3)]
        tk = ("T", 3 + (i % 3))
        rms_tile(x1[:, i, :], kA(2 * i, 2), P, ot[:, 0:D], [tk], 1)
        r0 = hf * TM + i * P
        plan.op("sp", lambda e: e.dma_start(out=out_d[r0:r0 + P, :], in_=ot[:, 0:D]),
                reads=[tk], writes=[("out", hf, i)], dma=True)

    def phase_D(hf, next_stream, cstream):
        for hh in range(2):
            def load_wd(hh=hh):
                plan.op("pool", lambda e: e.dma_start(
                    out=wdT[:, :, :], in_=w_down_d[hh * FH * P:(hh + 1) * FH * P, :].rearrange("(k p) n -> p k n", p=P)),
                    writes=kA(27, FH), dma=True)
            for cp in range(6):
                nco = SLOTW if cp < 5 else P
                c0 = hh * FH * P + cp * SLOTW
                sG = slot_load(w_gate_d, 0, KC, c0, nco)
                sU = slot_load(w_up_d, 0, KC, c0, nco)
                if cp == 2:
                    load_wd()
                for jj in range(nco // P):
                    c = cp * 2 + jj
                    sg = T[c % 2]
                    first = (hh == 0 and c == 0)
                    if first:
                        hg, hu = proj_begin(sG, 0), proj_begin(sU, 0)
                        cstream.finish()
                        u = proj_end(hg)
                    else:
                        u = proj_unit(sG, jj * P, False)
                    act(lambda e, u=u, sg=sg: e.activation(out=sg[:, 0:TM], in_=PU[u][:, 0:TM], func=AF.Silu),
                        reads=[("PU", u)], writes=[("T", c % 2)])
                    u = proj_end(hu) if first else proj_unit(sU, jj * P, False)
                    dve(lambda e, u=u, sg=sg, c=c: e.tensor_tensor(out=actT[:, c, :], in0=PU[u][:, 0:TM], in1=sg[:, 0:TM],
                                                                   op=ALU.mult),
                        reads=[("PU", u), ("T", c % 2)], writes=kA(16 + c))
            if hh == 1 and next_stream is not None:
                load_gain(0, 0)
                next_stream.advance()
            for i in range(NT):
                u = next_pu()

                def dgrp(kcs):
                    return [(PU[u][:, nb2 * 512:(nb2 + 1) * 512], actT[:, kc, i * P:(i + 1) * P],
                             wdT[:, kc, nb2 * 512:(nb2 + 1) * 512], kc == 0, kc == FH - 1)
                            for nb2 in range(2) for kc in kcs]
                if i == 0:
                    pe_unit(dgrp(range(FH - 1)), kA(16, FH - 1) + kA(27, FH), u)
                    pe_unit(dgrp([FH - 1]), kA(16 + FH - 1, 1) + kA(27, FH), u)
                else:
                    pe_unit(dgrp(range(FH)), kA(16, 2 * FH), u)
                dve(lambda e, u=u, i=i: e.tensor_tensor(out=x1[:, i, :], in0=PU[u][:, 0:D], in1=x1[:, i, :], op=ALU.add),
                    reads=[("PU", u)] + kA(2 * i, 2), writes=kA(2 * i, 2))
                if hh == 1:
                    if i >= 1:
                        final_tile(hf, i - 1)
                    if next_stream is not None:
                        next_stream.advance()
            if hh == 1:
                final_tile(hf, NT - 1)
            if hh == 1 and next_stream is not None:
                next_stream.drain_keep_last()

    load_gain(0, 0)
    a0 = make_A_stream(0, 2, staged=True)
    a0._loads(1)
    for hx in range(4):
        plan.op("sp", lambda e, hx=hx: e.dma_start(
            out=x1[:, hx * 2:(hx + 1) * 2, :],
            in_=x_d[0, HALO + hx * 256:HALO + (hx + 1) * 256, :].rearrange("(i p) f -> p i f", p=P)),
            writes=kA(hx * 4, 4), dma=True)
    phase_K_loads()
    for _ in range(7):
        a0.advance()
    sQ0 = slot_load(w_in_d, 0, KC, 4 * D, after=kA(0, 8))
    pre0 = (sQ0, [proj_begin(sQ0, 0), proj_begin(sQ0, P)])
    a0.finish()
    phase_K_norm1()
    dump("hT0", AR[:, 0:NH], kH, (P, NH), BF16)
    nxt = None
    for hf in range(2):
        phase_B(hf, nxt, pre0 if hf == 0 else None)
        cst = phase_B5_C(hf)
        if hf == 0:
            load_gain(1, 3)
        nxt = make_A_stream(1, 3) if hf == 0 else None
        phase_D(hf, nxt, cst)

    plan.op("sp", lambda e: e.nop(), reads=[("out", hf, i) for hf in range(2) for i in range(NT)])

    sems = {}
    for e in Plan.ENG:
        sems[e] = nc.alloc_semaphore("c_" + e)
    for q in ("sp", "pool"):
        for i in range(Plan.ND):
            sems[("dma", q, i)] = nc.alloc_semaphore("d_%s%d" % (q, i))

    def runner(ename):
        def f(eng):
            for waits, fn, semkey, inc in plan.ops[ename]:
                for sk, v in waits:
                    eng.wait_ge(sems[sk], v)
                ins = fn(eng)
                ins.then_inc(sems[semkey], inc)
        return f

    with nc.Block() as block:
        block.tensor(runner("pe"))
        block.scalar(runner("act"))
        block.vector(runner("dve"))
        block.gpsimd(runner("pool"))
        block.sync(runner("sp"))
    return nc, dbg


def _host_inputs(inputs):
    f = lambda a: np.ascontiguousarray(np.asarray(a, dtype=np.float32))
    x = f(inputs["x"])
    mem = f(inputs["mem"])
    B, S, _ = x.shape
    gains = np.stack([np.broadcast_to(f(inputs[k]).reshape(1, D), (P, D))
                      for k in ("norm_mix", "norm_mem", "norm_ffn", "norm_final")]).astype(np.float32)
    conv_w = f(inputs["conv_w"]).reshape(3, 8, P)
    pscale = f(inputs["pool_scale"]).reshape(8, P)
    ident = np.eye(P, dtype=np.float32)
    shared = {
        "gains": np.ascontiguousarray(gains), "ident": ident,
        "w_in": f(inputs["w_in"])[0], "w_conv_out": f(inputs["w_conv_out"])[0], "w_pool": f(inputs["w_pool"])[0],
        "w_kv": f(inputs["w_kv"])[0], "w_xattn_out": f(inputs["w_xattn_out"])[0], "w_out": f(inputs["w_out"])[0],
        "w_gate": f(inputs["w_gate"])[0], "w_up": f(inputs["w_up"])[0], "w_down": f(inputs["w_down"])[0],
    }
    in_maps = []
    for c in range(N_CORES):
        b, half = c // 2, c % 2
        xs = np.zeros((2, TH, D), np.float32)
        small = np.zeros((P, 160), np.float32)
        small[:, 0:24] = conv_w.transpose(2, 1, 0).reshape(P, 24)
        small[:, 24:32] = pscale.T
        for hf in range(2):
            t0 = half * 2 * TM + hf * TM
            lo = t0 - HALO
            if lo >= 0:
                xs[hf] = x[b, lo:t0 + TM]
            else:
                xs[hf, HALO:] = x[b, t0:t0 + TM]
            for g in range(4):
                w = 2 << g
                pos = t0 + np.arange(16) + 1
                cnt = np.minimum(pos, w).astype(np.float32)
                small[:, 32 + hf * 64 + g * 16: 32 + hf * 64 + (g + 1) * 16] = (np.float32(1.0) / cnt)[None, :]
        m = {"x": xs, "mem": np.ascontiguousarray(mem[b]), "small": small}
        m.update(shared)
        in_maps.append(m)
    return in_maps, (B, S)


_CACHE = {}


def kernel(**inputs):
    in_maps, (B, S) = _host_inputs(inputs)
    if "nc" not in _CACHE:
        _CACHE["nc"] = build_program(debug=False)[0]
    nc = _CACHE["nc"]
    res = run_bass_kernel_spmd(nc, in_maps, core_ids=list(range(N_CORES)))
    out = np.empty((B, S, D), np.float32)
    for c in range(N_CORES):
        b, half = c // 2, c % 2
        out[b, half * 2 * TM:(half + 1) * 2 * TM] = np.asarray(res.results[c]["out"], dtype=np.float32)
    return out
```

```python
import numpy as np
import concourse.bass as bass
import concourse.mybir as mybir
from concourse.bass_utils import run_bass_kernel_spmd

F32 = mybir.dt.float32
BF16 = mybir.dt.bfloat16
AF = mybir.ActivationFunctionType
ALU = mybir.AluOpType

P = 128
D = 1024
KC = 8
TM = 1024
HALO = 16
TH = TM + HALO
NT = TM // P
DIN = 8192
DFF = 2816
FH = 11
NMEM = 256
SLOTW = 256
RING = 12
POOL_ADD_ENGINE = "dve"
EPS = 1e-6
N_CORES = 8


class Plan:
    ENG = ("pe", "act", "dve", "pool", "sp")
    ND = 8

    def __init__(self):
        self.ops = {e: [] for e in self.ENG}
        self.cnt = {e: 0 for e in self.ENG}
        self.seen = {e: {} for e in self.ENG}
        self.last_w = {}
        self.readers = {}
        self.dma_n = {"sp": 0, "pool": 0}
        self.dma_uses = {}

    def op(self, eng, fn, reads=(), writes=(), dma=False):
        deps = set()
        for k in reads:
            if k in self.last_w:
                deps.add(self.last_w[k])
        for k in writes:
            if k in self.last_w:
                deps.add(self.last_w[k])
            deps.update(self.readers.get(k, ()))
        if dma:
            i = self.dma_n[eng] % self.ND
            self.dma_n[eng] += 1
            semkey = ("dma", eng, i)
            uses = self.dma_uses.get(semkey, 0)
            if uses > 0:
                deps.add((semkey, 16 * uses))
            self.dma_uses[semkey] = uses + 1
            token = (semkey, 16 * (uses + 1))
        else:
            semkey = eng
            self.cnt[eng] += 1
            token = (eng, self.cnt[eng])
        best = {}
        for sk, v in deps:
            if v > best.get(sk, 0):
                best[sk] = v
        waits = []
        for sk, v in best.items():
            if sk == "pe" and eng == "pe":
                continue
            if self.seen[eng].get(sk, 0) >= v:
                continue
            self.seen[eng][sk] = v
            waits.append((sk, v))
        self.ops[eng].append((waits, fn, semkey, 16 if dma else 1))
        for k in reads:
            self.readers.setdefault(k, []).append(token)
        for k in writes:
            self.last_w[k] = token
            self.readers[k] = []
        return token


def build_program(debug=False):
    nc = bass.Bass("TRN2", target_bir_lowering=False)

    def din(name, shape):
        return nc.dram_tensor(name, list(shape), F32, kind="ExternalInput")

    x_d = din("x", (2, TH, D))
    mem_d = din("mem", (NMEM, D))
    gains_d = din("gains", (4, P, D))
    small_d = din("small", (P, 160))
    ident_d = din("ident", (P, P))
    w_in_d = din("w_in", (D, DIN))
    w_co_d = din("w_conv_out", (D, D))
    w_pool_d = din("w_pool", (4, 256, 256))
    w_kv_d = din("w_kv", (D, 2 * D))
    w_xo_d = din("w_xattn_out", (D, D))
    w_out_d = din("w_out", (D, D))
    w_gate_d = din("w_gate", (D, DFF))
    w_up_d = din("w_up", (D, DFF))
    w_down_d = din("w_down", (DFF, D))
    out_d = nc.dram_tensor("out", [2 * TM, D], F32, kind="ExternalOutput")
    dbg = {}

    def sb(name, shape, dt):
        return nc.alloc_sbuf_tensor(name, list(shape), dt)

    ring = sb("ring", (P, RING, KC, SLOTW), BF16)
    NH = KC * TH
    NA = 38 * 1024
    AR = sb("arena", (P, NH + NA), BF16)
    hT = AR[:, 0:NH].rearrange("p (k t) -> p k t", k=KC)

    def atoms(a0, n):
        return AR[:, NH + a0 * 1024: NH + (a0 + n) * 1024]

    zT = atoms(0, 8).rearrange("p (k t) -> p k t", k=8)
    plT = atoms(8, 8).rearrange("p (k t) -> p k t", k=8)
    oT = atoms(16, 8).rearrange("p (k t) -> p k t", k=8)
    qT = atoms(24, 8).rearrange("p (k t) -> p k t", k=8)
    mT = qT
    x1 = atoms(0, 16).bitcast(F32).rearrange("p (i f) -> p i f", i=NT)
    actT = atoms(16, FH).rearrange("p (k t) -> p k t", k=FH)
    wdT = atoms(27, FH).rearrange("p (k t) -> p k t", k=FH)

    def kA(a0, n=1):
        return [("A", a) for a in range(a0, a0 + n)]

    kH = [("Ht", t) for t in range(1, NT + 1)]

    def kHt(c0):
        return [("Ht", 0 if c0 == 0 else 1 + (c0 - HALO) // P)]

    T = [sb("tmp%d" % i, (P, TH), F32) for i in range(6)]
    gBs = [sb("gB%d" % i, (P, D), F32) for i in range(2)]
    hb = [sb("hb%d" % i, (P, D), BF16) for i in range(4)]
    junk = sb("junk", (P, D), BF16)
    kT = sb("kT", (P, KC, NMEM), BF16)
    vv = sb("vv", (P, 2, D), BF16)
    memT = T[5][:, 0:1024].bitcast(BF16).rearrange("p (k t) -> p k t", k=KC)
    eT = [sb("eT%d" % i, (P, 1024), BF16) for i in range(2)]
    rden = [sb("rden%d" % i, (P, 512), F32) for i in range(2)]
    wpool = sb("wpool", (P, 4, 2, 256), BF16)
    ident = sb("identb", (P, P), BF16)
    ones = sb("onesb", (P, P), BF16)
    small = sb("smallc", (P, 160), F32)
    stat = sb("stat", (P, 96), F32)
    tmp16 = sb("tmp16", (P, 16), F32)
    pst = sb("pstash", (P, KC, HALO), F32)
    ust = sb("ustash", (P, KC, HALO), F32)

    PS = nc.alloc_psum_tensor("ps", [P, 4096], F32)
    NPU = 3
    PU = [PS[:, u * 1024:(u + 1) * 1024] for u in range(NPU)]
    HS = PS[:, 3072:3584]
    DEN = PS[:, 3584:4096]
    PTv = [PU[u].bitcast(BF16)[:, 0:1024].rearrange("p (k t) -> p k t", k=8) for u in range(NPU)]

    plan = Plan()
    st = {"slot": 0, "pu": 0, "hs": 0, "stat": 0, "hb": 0}

    def dump(name, ap, keys, shape, dt):
        if not debug:
            return
        d = nc.dram_tensor("dbg_" + name, list(shape), dt, kind="ExternalOutput")
        dbg[name] = d
        plan.op("sp", lambda e, d=d, ap=ap: e.dma_start(out=d.ap(), in_=ap), reads=keys,
                writes=[("dbg", name)], dma=True)

    def slot_load(W, r0, nk, c0, ncols=SLOTW, after=()):
        s = st["slot"] % RING
        st["slot"] += 1
        src = W[r0:r0 + nk * P, c0:c0 + ncols].rearrange("(k p) n -> p k n", p=P)
        dst = ring[:, s, 0:nk, 0:ncols]
        plan.op("pool", lambda e: e.dma_start(out=dst, in_=src), reads=list(after), writes=[("S", s)], dma=True)
        return s

    def pe_unit(mms, reads, u=None, writes=None):
        def fn(e):
            ins = None
            for (o, l, r, s0, s1) in mms:
                ins = e.matmul(o, l, r, start=s0, stop=s1)
            return ins
        w = writes if writes is not None else [("PU", u)]
        plan.op("pe", fn, reads=reads, writes=w)

    held = set()

    def next_pu():
        while True:
            u = st["pu"] % NPU
            st["pu"] += 1
            if u not in held:
                return u

    def proj_unit(slot, colofs, halo, src=None, srckeys=None, split=False):
        own = src is None
        src = hT if src is None else src
        u = next_pu()
        hp = None
        if halo:
            hp = st["hs"] % 32
            st["hs"] += 1
        blk = [[], []]
        for kc in range(KC):
            lw = ring[:, slot, kc, colofs:colofs + P]
            blk[0].append((PU[u][:, 0:512], lw, src[:, kc, 16:528], kc == 0, kc == KC - 1))
            blk[1].append((PU[u][:, 512:1024], lw, src[:, kc, 528:1040], kc == 0, kc == KC - 1))
            if halo:
                blk[1].append((HS[:, hp * 16:(hp + 1) * 16], lw, src[:, kc, 0:16], kc == 0, kc == KC - 1))
        wr = [("PU", u)] + ([("HS",)] if halo else [])
        if own:
            k0, k1 = kH[0:4], kH[4:8] + ([("Ht", 0)] if halo else [])
        else:
            k0 = k1 = srckeys
        if split:
            pe_unit(blk[0], [("S", slot)] + k0, writes=[("PU", u)])
            pe_unit(blk[1], [("S", slot)] + k1, writes=wr)
        else:
            mms = []
            n1 = len(blk[1]) // KC
            for kc in range(KC):
                mms.append(blk[0][kc])
                mms.extend(blk[1][kc * n1:(kc + 1) * n1])
            pe_unit(mms, [("S", slot)] + list(k0) + [k for k in k1 if k not in k0], writes=wr)
        return (u, hp) if halo else u

    def proj_begin(slot, colofs):
        u = next_pu()
        mms = [(PU[u][:, 0:512], ring[:, slot, kc, colofs:colofs + P], hT[:, kc, 16:528], kc == 0, kc == KC - 1)
               for kc in range(KC)]
        pe_unit(mms, [("S", slot)] + kH[0:4], writes=[("PU", u)])
        held.add(u)
        return (u, slot, colofs)

    def proj_end(h):
        u, slot, colofs = h
        mms = [(PU[u][:, 512:1024], ring[:, slot, kc, colofs:colofs + P], hT[:, kc, 528:1040], kc == 0, kc == KC - 1)
               for kc in range(KC)]
        pe_unit(mms, [("S", slot)] + kH[4:8], writes=[("PU", u)])
        held.discard(u)
        return u

    def act(fn, reads, writes):
        plan.op("act", fn, reads=reads, writes=writes)

    def dve(fn, reads, writes):
        plan.op("dve", fn, reads=reads, writes=writes)

    def gps(fn, reads, writes):
        plan.op(POOL_ADD_ENGINE, fn, reads=reads, writes=writes)

    def new_stat():
        c = st["stat"] % 32
        st["stat"] += 1
        return c

    def rms_tile(src_ap, src_keys, rows, dst_ap, dst_keys, gi):
        c = new_stat()
        gB = gBs[gi]
        ss, sd, rs = stat[:rows, c:c + 1], stat[:rows, 32 + c:33 + c], stat[:rows, 64 + c:65 + c]
        act(lambda e: e.activation(out=junk[:rows, :], in_=src_ap, func=AF.Square, accum_out=ss),
            reads=src_keys, writes=[("junk",), ("st", c)])
        act(lambda e: e.activation(out=sd, in_=ss, func=AF.Sqrt, bias=EPS, scale=1.0 / D),
            reads=[("st", c)], writes=[("st", 32 + c)])
        dve(lambda e: e.reciprocal(out=rs, in_=sd), reads=[("st", 32 + c)], writes=[("st", 64 + c)])
        dve(lambda e: e.scalar_tensor_tensor(out=dst_ap, in0=src_ap, scalar=rs, in1=gB[:rows, :],
                                             op0=ALU.mult, op1=ALU.mult),
            reads=src_keys + [("st", 64 + c), ("gB", gi)], writes=dst_keys)

    def transpose_tile(src_bf, src_keys, rows, dstT, c0, dst_keys, on_dve=False):
        m = next_pu()

        def fn(e):
            ins = None
            for kc in range(KC):
                ins = e.transpose(PTv[m][:, kc, 0:rows], src_bf[:rows, kc * P:(kc + 1) * P], ident[:rows, :rows])
            return ins
        plan.op("pe", fn, reads=src_keys + [("ident",)], writes=[("PU", m)])
        if on_dve:
            dve(lambda e: e.tensor_copy(out=dstT[:, :, c0:c0 + rows], in_=PTv[m][:, :, 0:rows]),
                reads=[("PU", m)], writes=dst_keys)
        else:
            act(lambda e: e.activation(out=dstT[:, :, c0:c0 + rows], in_=PTv[m][:, :, 0:rows], func=AF.Copy),
                reads=[("PU", m)], writes=dst_keys)

    def load_gain(gi, which):
        plan.op("sp", lambda e: e.dma_start(out=gBs[gi][:, :], in_=gains_d[which]), writes=[("gB", gi)], dma=True)

    class Stream:
        def __init__(self, tiles, skew, lead=2):
            self.tiles, self.skew, self.lead = tiles, skew, lead
            self.i, self.l, self.pending = 0, 0, []

        def _loads(self, upto):
            while self.l < min(upto, len(self.tiles)):
                s0 = self.tiles[self.l][0]
                if s0 is not None:
                    s0()
                self.l += 1

        def advance(self):
            self._loads(self.i + 1 + self.lead)
            if self.i < len(self.tiles):
                _, s1, s2 = self.tiles[self.i]
                self.i += 1
                self.pending.append((s2, s1()))
            if len(self.pending) > self.skew or (self.i >= len(self.tiles) and self.pending):
                s2, b = self.pending.pop(0)
                s2(b)

        def finish(self):
            while self.i < len(self.tiles) or self.pending:
                self.advance()

        def drain_keep_last(self):
            self._loads(len(self.tiles))
            while self.i < len(self.tiles):
                _, s1, s2 = self.tiles[self.i]
                self.i += 1
                self.pending.append((s2, s1()))
            while len(self.pending) > 1:
                s2, b = self.pending.pop(0)
                s2(b)

    def next_hb():
        b = st["hb"] % 4
        st["hb"] += 1
        return b

    plan.op("sp", lambda e: e.dma_start(out=small[:, :], in_=small_d.ap()), writes=[("small",)], dma=True)
    plan.op("pool", lambda e: e.dma_start(out=ident[:, :], in_=ident_d.ap()), writes=[("ident",)], dma=True)
    plan.op("pool", lambda e: e.dma_start(out=wpool[:, :, :, :],
                                          in_=w_pool_d.ap().rearrange("g (k p) n -> p g k n", p=P)),
            writes=[("wpool",)], dma=True)
    dve(lambda e: e.memset(ones[:, :], 1.0), reads=[], writes=[("ones",)])
    dve(lambda e: e.memset(tmp16[:, :], 1.0), reads=[], writes=[("tmp16",)])
    act(lambda e: e.activation(out=tmp16[:, 0:1], in_=tmp16[:, 1:2], func=AF.Sqrt),
        reads=[("tmp16",)], writes=[("tmp16",)])

    def cw(j, k):
        return small[:, j * 3 + k: j * 3 + k + 1]

    def pscale(j):
        return small[:, 24 + j: 25 + j]

    def invc(hf, g):
        return small[:, 32 + hf * 64 + g * 16: 32 + hf * 64 + (g + 1) * 16]

    def _shift(t3):
        class _V:
            def __getitem__(self, key):
                p, k, sl = key
                return t3[p, k, sl.start - HALO: sl.stop - HALO]
        return _V()

    def make_A_stream(hf, skew, staged=False):
        tiles = []
        for i in range(1 if hf == 1 else 0, NT + 1):
            rows = HALO if i == 0 else P
            r0 = 0 if i == 0 else HALO + (i - 1) * P
            from_x1 = staged and i > 0

            def s0(i=i, rows=rows, r0=r0):
                xt = T[i % 3]
                plan.op("sp", lambda e: e.dma_start(out=xt[:rows, 0:D], in_=x_d[hf, r0:r0 + rows, :]),
                        writes=[("T", i % 3)], dma=True)

            def s1(i=i, rows=rows, r0=r0, from_x1=from_x1):
                b = next_hb()
                if from_x1:
                    rms_tile(x1[:, i - 1, :], kA(2 * (i - 1), 2), rows, hb[b][:rows, :], [("hb", b)], 0)
                else:
                    xt = T[i % 3]
                    rms_tile(xt[:rows, 0:D], [("T", i % 3)], rows, hb[b][:rows, :], [("hb", b)], 0)
                return b

            def s2(b, i=i, rows=rows, r0=r0):
                transpose_tile(hb[b], [("hb", b)], rows, hT, r0, kHt(r0), on_dve=(staged and i % 2 == 1))
            tiles.append((None if from_x1 else s0, s1, s2))
        return Stream(tiles, skew)

    def make_C_stream(skew):
        tiles = []
        for i in range(NT):
            def s1(i=i):
                b = next_hb()
                rms_tile(x1[:, i, :], kA(2 * i, 2), P, hb[b][:, :], [("hb", b)], 0)
                return b

            def s2(b, i=i):
                transpose_tile(hb[b], [("hb", b)], P, hT, HALO + i * P, kHt(HALO + i * P))
            tiles.append((None, s1, s2))
        return Stream(tiles, skew)

    def phase_K_loads():
        load_gain(1, 1)
        for mt in range(2):
            xt = T[3 + mt]
            plan.op("sp", lambda e, xt=xt, mt=mt: e.dma_start(out=xt[:, 0:D], in_=mem_d[mt * P:(mt + 1) * P, :]),
                    writes=[("T", 3 + mt)], dma=True)

    kn = {"hb": []}

    def phase_K_norm1():
        for mt in range(2):
            xt = T[3 + mt]
            b = next_hb()
            rms_tile(xt[:, 0:D], [("T", 3 + mt)], P, hb[b][:, :], [("hb", b)], 1)
            kn["hb"].append(b)

    def phase_K_norm2():
        for mt in range(2):
            b = kn["hb"][mt]
            transpose_tile(hb[b], [("hb", b)], P, memT, mt * P, [("T", 5)])

    def phase_K_units():
        for g in range(4):
            s = slot_load(w_kv_d, 0, KC, g * SLOTW)
            for jj in range(2):
                j = g * 2 + jj
                u = next_pu()
                mms = [(PU[u][:, 0:NMEM], ring[:, s, kc, jj * P:(jj + 1) * P], memT[:, kc, :], kc == 0, kc == KC - 1)
                       for kc in range(KC)]
                pe_unit(mms, [("S", s), ("T", 5)], u)
                act(lambda e, u=u, j=j: e.activation(out=kT[:, j, :], in_=PU[u][:, 0:NMEM], func=AF.Copy),
                    reads=[("PU", u)], writes=[("kT",)])
        assert st["slot"] % 2 == 0
        for gp in range(2):
            s0 = slot_load(w_kv_d, 0, KC, D + (2 * gp) * SLOTW)
            s1 = slot_load(w_kv_d, 0, KC, D + (2 * gp + 1) * SLOTW)
            for mt in range(2):
                u = next_pu()
                mms = [(PU[u][:, 0:512], memT[:, kc, mt * P:(mt + 1) * P], ring[:, s0:s0 + 2, kc, :], kc == 0, kc == KC - 1)
                       for kc in range(KC)]
                pe_unit(mms, [("S", s0), ("S", s1), ("T", 5)], u)
                act(lambda e, u=u, mt=mt, gp=gp: e.activation(out=vv[:, mt, gp * 512:(gp + 1) * 512],
                                                              in_=PU[u][:, 0:512], func=AF.Copy),
                    reads=[("PU", u)], writes=[("vv",)])
        dump("kT", kT[:, :, :], [("kT",)], (P, KC, NMEM), BF16)
        dump("vv", vv[:, :, :], [("vv",)], (P, 2, D), BF16)

    def conv_chunk_a(hf, j, sC, jj):
        ca = T[0]
        if hf == 1:
            u = proj_unit(sC, jj * P, False)
            act(lambda e: e.activation(out=ca[:, HALO:TH], in_=PU[u][:, 0:TM], func=AF.Copy),
                reads=[("PU", u)], writes=[("T", 0)])
            return
        u, hp = proj_unit(sC, jj * P, True)
        act(lambda e: e.activation(out=ca[:, 0:HALO], in_=HS[:, hp * 16:(hp + 1) * 16], func=AF.Copy),
            reads=[("HS",)], writes=[("T", 0)])
        act(lambda e: e.activation(out=ca[:, HALO:TH], in_=PU[u][:, 0:TM], func=AF.Copy),
            reads=[("PU", u), ("T", 0)], writes=[("T", 0)])

    def conv_chunk_b(hf, j, sU, jj):
        ca, pp, cv = T[0], T[1], T[2]
        if hf == 1:
            u2 = proj_unit(sU, jj * P, False)
            dve(lambda e: e.tensor_copy(out=pp[:, 0:HALO], in_=pst[:, j, :]),
                reads=[("pst", j)], writes=[("T", 1)])
        else:
            u2, hp2 = proj_unit(sU, jj * P, True)
            dve(lambda e: e.tensor_tensor(out=pp[:, 0:HALO], in0=HS[:, hp2 * 16:(hp2 + 1) * 16], in1=ca[:, 0:HALO],
                                          op=ALU.mult),
                reads=[("HS",), ("T", 0)], writes=[("T", 1)])
        dve(lambda e: e.tensor_tensor(out=pp[:, HALO:TH], in0=PU[u2][:, 0:TM], in1=ca[:, HALO:TH], op=ALU.mult),
            reads=[("PU", u2), ("T", 0), ("T", 1)], writes=[("T", 1)])
        if hf == 0:
            dve(lambda e: e.tensor_copy(out=pst[:, j, :], in_=pp[:, TH - HALO:TH]),
                reads=[("T", 1)], writes=[("pst", j)])
        act(lambda e: e.mul(out=cv[:, 0:TM], in_=pp[:, 14:14 + TM], mul=cw(j, 0)),
            reads=[("T", 1), ("small",)], writes=[("T", 2)])
        dve(lambda e: e.scalar_tensor_tensor(out=cv[:, 0:TM], in0=pp[:, 15:15 + TM], scalar=cw(j, 1),
                                             in1=cv[:, 0:TM], op0=ALU.mult, op1=ALU.add),
            reads=[("T", 1), ("T", 2), ("small",)], writes=[("T", 2)])
        dve(lambda e: e.scalar_tensor_tensor(out=cv[:, 0:TM], in0=pp[:, 16:16 + TM], scalar=cw(j, 2),
                                             in1=cv[:, 0:TM], op0=ALU.mult, op1=ALU.add),
            reads=[("T", 1), ("T", 2), ("small",)], writes=[("T", 2)])

    def conv_chunk2(j, sB, jj):
        cv = T[2]
        u3 = proj_unit(sB, jj * P, False)
        dve(lambda e: e.tensor_tensor(out=zT[:, j, :], in0=PU[u3][:, 0:TM], in1=cv[:, 0:TM], op=ALU.mult),
            reads=[("PU", u3), ("T", 2)], writes=kA(j))

    def pool_chunk(hf, j, sP, jj):
        up, sA, sBt = T[3], T[4], T[5]
        grp = j // 2
        w = 2 << grp
        if hf == 1:
            u = proj_unit(sP, jj * P, False)
            act(lambda e: e.activation(out=up[:, 0:HALO], in_=ust[:, j, :], func=AF.Copy),
                reads=[("ust", j)], writes=[("T", 3)])
        else:
            u, hp = proj_unit(sP, jj * P, True)
            act(lambda e: e.activation(out=up[:, 0:HALO], in_=HS[:, hp * 16:(hp + 1) * 16], func=AF.Copy),
                reads=[("HS",)], writes=[("T", 3)])
        act(lambda e: e.activation(out=up[:, HALO:TH], in_=PU[u][:, 0:TM], func=AF.Copy),
            reads=[("PU", u), ("T", 3)], writes=[("T", 3)])
        if hf == 0:
            act(lambda e: e.activation(out=ust[:, j, :], in_=up[:, TH - HALO:TH], func=AF.Copy),
                reads=[("T", 3)], writes=[("ust", j)])
        gps(lambda e: e.tensor_tensor(out=sA[:, 1:TH], in0=up[:, 1:TH], in1=up[:, 0:TH - 1], op=ALU.add),
            reads=[("T", 3)], writes=[("T", 4)])
        S, Sk = sA, 4
        if w >= 4:
            gps(lambda e: e.tensor_tensor(out=sBt[:, 3:TH], in0=sA[:, 3:TH], in1=sA[:, 1:TH - 2], op=ALU.add),
                reads=[("T", 4)], writes=[("T", 5)])
            S, Sk = sBt, 5
        if w >= 8:
            gps(lambda e: e.tensor_tensor(out=sA[:, 7:TH], in0=sBt[:, 7:TH], in1=sBt[:, 3:TH - 4], op=ALU.add),
                reads=[("T", 5)], writes=[("T", 4)])
            S, Sk = sA, 4
        if w >= 16:
            gps(lambda e: e.tensor_tensor(out=sBt[:, 15:TH], in0=sA[:, 15:TH], in1=sA[:, 7:TH - 8], op=ALU.add),
                reads=[("T", 4)], writes=[("T", 5)])
            S, Sk = sBt, 5
        dve(lambda e: e.scalar_tensor_tensor(out=plT[:, j, :], in0=S[:, HALO:TH], scalar=1.0 / w,
                                             in1=up[:, HALO:TH], op0=ALU.mult, op1=ALU.subtract),
            reads=[("T", Sk), ("T", 3)], writes=kA(8 + j))
        dve(lambda e: e.tensor_tensor(out=tmp16[:, :], in0=S[:, HALO:2 * HALO], in1=invc(hf, grp), op=ALU.mult),
            reads=[("T", Sk), ("small",)], writes=[("tmp16",)])
        dve(lambda e: e.tensor_tensor(out=plT[:, j, 0:HALO], in0=tmp16[:, :], in1=up[:, HALO:2 * HALO],
                                      op=ALU.subtract),
            reads=[("tmp16",), ("T", 3)], writes=kA(8 + j))

    def q_chunk(j, sQ, jj):
        u = proj_unit(sQ, jj * P, False, split=(j < 2))
        act(lambda e, u=u: e.activation(out=qT[:, j, :], in_=PU[u][:, 0:TM], func=AF.Copy),
            reads=[("PU", u)], writes=kA(24 + j))

    def attn_S(it):
        nb, h = it % 2, it // 2
        b = it % 2
        u = next_pu()
        mms = []
        for mc in range(2):
            for dc in range(2):
                mms.append((PU[u][:, mc * 512:(mc + 1) * 512], kT[:, 2 * h + dc, mc * P:(mc + 1) * P],
                            qT[:, 2 * h + dc, nb * 512:(nb + 1) * 512], dc == 0, dc == 1))
        pe_unit(mms, [("kT",)] + kA(24 + 2 * h, 2), u)
        act(lambda e: e.activation(out=eT[b][:, :], in_=PU[u][:, 0:1024], func=AF.Exp, scale=1.0 / 16.0),
            reads=[("PU", u)], writes=[("eT", b)])

    def attn_O(it):
        nb, h = it % 2, it // 2
        b = it % 2
        u2 = next_pu()
        mms = []
        for dc in range(2):
            for mc in range(2):
                mms.append((PU[u2][:, dc * 512:(dc + 1) * 512], vv[:, mc, h * 256 + dc * P: h * 256 + (dc + 1) * P],
                            eT[b][:, mc * 512:(mc + 1) * 512], mc == 0, mc == 1))
        for mc in range(2):
            mms.append((DEN[:, :], ones[:, :], eT[b][:, mc * 512:(mc + 1) * 512], mc == 0, mc == 1))
        pe_unit(mms, [("vv",), ("ones",), ("eT", b)], writes=[("PU", u2), ("DEN",)])
        dve(lambda e: e.reciprocal(out=rden[b][:, :], in_=DEN[:, :]),
            reads=[("DEN",)], writes=[("rden", b)])
        for dc in range(2):
            dve(lambda e, dc=dc: e.tensor_tensor(
                out=oT[:, 2 * h + dc, nb * 512:(nb + 1) * 512], in0=PU[u2][:, dc * 512:(dc + 1) * 512],
                in1=rden[b][:, :], op=ALU.mult),
                reads=[("PU", u2), ("rden", b)], writes=kA(16 + 2 * h + dc))

    def merge_chunk(j, sGa, sCo, sGp, sGx, sXo, jj):
        sa, sp_, sx, mm_, t1, t2 = T[0], T[1], T[2], T[3], T[4], T[5]
        grp = j // 2
        u = proj_unit(sGa, jj * P, False)
        act(lambda e, u=u: e.activation(out=sa[:, 0:TM], in_=PU[u][:, 0:TM], func=AF.Sigmoid),
            reads=[("PU", u)], writes=[("T", 0)])
        u = proj_unit(sCo, jj * P, False, src=_shift(zT), srckeys=kA(0, 8))
        dve(lambda e, u=u: e.tensor_tensor(out=mm_[:, 0:TM], in0=PU[u][:, 0:TM], in1=sa[:, 0:TM], op=ALU.mult),
            reads=[("PU", u), ("T", 0)], writes=[("T", 3)])
        u = proj_unit(sGp, jj * P, False)
        act(lambda e, u=u: e.activation(out=sp_[:, 0:TM], in_=PU[u][:, 0:TM], func=AF.Sigmoid),
            reads=[("PU", u)], writes=[("T", 1)])
        u = next_pu()
        mms = []
        for k2 in range(2):
            for nbk in range(2):
                mms.append((PU[u][:, nbk * 512:(nbk + 1) * 512], wpool[:, grp, k2, (j % 2) * P:(j % 2 + 1) * P],
                            plT[:, 2 * grp + k2, nbk * 512:(nbk + 1) * 512], k2 == 0, k2 == 1))
        pe_unit(mms, [("wpool",)] + kA(8 + 2 * grp, 2), u)
        dve(lambda e, u=u: e.scalar_tensor_tensor(out=t1[:, 0:TM], in0=PU[u][:, 0:TM], scalar=pscale(j),
                                                  in1=sp_[:, 0:TM], op0=ALU.mult, op1=ALU.mult),
            reads=[("PU", u), ("T", 1), ("small",)], writes=[("T", 4)])
        dve(lambda e: e.tensor_tensor(out=mm_[:, 0:TM], in0=mm_[:, 0:TM], in1=t1[:, 0:TM], op=ALU.add),
            reads=[("T", 3), ("T", 4)], writes=[("T", 3)])
        u = proj_unit(sGx, jj * P, False)
        act(lambda e, u=u: e.activation(out=sx[:, 0:TM], in_=PU[u][:, 0:TM], func=AF.Sigmoid),
            reads=[("PU", u)], writes=[("T", 2)])
        u = proj_unit(sXo, jj * P, False, src=_shift(oT), srckeys=kA(16, 8))
        dve(lambda e, u=u: e.tensor_tensor(out=t2[:, 0:TM], in0=PU[u][:, 0:TM], in1=sx[:, 0:TM], op=ALU.mult),
            reads=[("PU", u), ("T", 2)], writes=[("T", 5)])
        dve(lambda e: e.tensor_tensor(out=mT[:, j, :], in0=mm_[:, 0:TM], in1=t2[:, 0:TM], op=ALU.add),
            reads=[("T", 3), ("T", 5)], writes=kA(24 + j))

    def phase_B(hf, instream=None):
        for g in range(4):
            sQ = slot_load(w_in_d, 0, KC, 4 * D + g * SLOTW, after=kA(0, 8) if (hf == 0 and g == 0) else ())
            if g == 0 and instream is not None:
                hq = [proj_begin(sQ, 0), proj_begin(sQ, P)]
                instream.finish()
                for jj in range(2):
                    u = proj_end(hq[jj])
                    act(lambda e, u=u, jj=jj: e.activation(out=qT[:, jj, :], in_=PU[u][:, 0:TM], func=AF.Copy),
                        reads=[("PU", u)], writes=kA(24 + jj))
                continue
            for jj in range(2):
                q_chunk(g * 2 + jj, sQ, jj)
            if hf == 0 and g == 1:
                phase_K_norm2()
        dump("qT%d" % hf, atoms(24, 8), kA(24, 8), (P, 8192), BF16)
        def load_cp(g):
            return (slot_load(w_in_d, 0, KC, 1 * D + g * SLOTW), slot_load(w_in_d, 0, KC, 2 * D + g * SLOTW),
                    slot_load(w_in_d, 0, KC, 0 * D + g * SLOTW), slot_load(w_in_d, 0, KC, 3 * D + g * SLOTW))

        def load_mg(g):
            return (slot_load(w_in_d, 0, KC, 5 * D + g * SLOTW), slot_load(w_co_d, 0, KC, g * SLOTW),
                    slot_load(w_in_d, 0, KC, 6 * D + g * SLOTW), slot_load(w_in_d, 0, KC, 7 * D + g * SLOTW),
                    slot_load(w_xo_d, 0, KC, g * SLOTW))
        if hf == 0:
            phase_K_units()
        nxt = load_cp(0)
        mg0 = None
        for g in range(4):
            sC, sU, sB, sP = nxt
            if g < 3:
                nxt = load_cp(g + 1)
            else:
                mg0 = load_mg(0)
            for jj in range(2):
                j = g * 2 + jj
                conv_chunk_a(hf, j, sC, jj)
                attn_S(j)
                conv_chunk_b(hf, j, sU, jj)
                conv_chunk2(j, sB, jj)
                attn_O(j)
                pool_chunk(hf, j, sP, jj)
        dump("zT%d" % hf, atoms(0, 8), kA(0, 8), (P, 8192), BF16)
        dump("plT%d" % hf, atoms(8, 8), kA(8, 8), (P, 8192), BF16)
        dump("oT%d" % hf, atoms(16, 8), kA(16, 8), (P, 8192), BF16)
        for g in range(4):
            sGa, sCo, sGp, sGx, sXo = mg0 if g == 0 else load_mg(g)
            for jj in range(2):
                merge_chunk(g * 2 + jj, sGa, sCo, sGp, sGx, sXo, jj)
        dump("mT%d" % hf, atoms(24, 8), kA(24, 8), (P, 8192), BF16)

    def phase_B5_C(hf):
        load_gain(0, 2)
        for hx in range(2):
            plan.op("sp", lambda e, hx=hx: e.dma_start(
                out=x1[:, hx * 4:(hx + 1) * 4, :],
                in_=x_d[hf, HALO + hx * 512:HALO + (hx + 1) * 512, :].rearrange("(i p) f -> p i f", p=P)),
                writes=kA(hx * 8, 8), dma=True)
        assert st["slot"] % 2 == 0
        sO = [slot_load(w_out_d, 0, KC, g * SLOTW) for g in range(4)]
        cs = make_C_stream(2)
        for i in range(NT):
            u = next_pu()

            def grp(kcs):
                mms = []
                for g2 in range(2):
                    for kc in kcs:
                        mms.append((PU[u][:, g2 * 512:(g2 + 1) * 512], mT[:, kc, i * P:(i + 1) * P],
                                    ring[:, sO[2 * g2]:sO[2 * g2] + 2, kc, :], kc == 0, kc == KC - 1))
                return mms
            if i == 0:
                pe_unit(grp(range(KC - 1)), [("S", s) for s in sO] + kA(24, 7), u)
                pe_unit(grp([KC - 1]), [("S", s) for s in sO] + kA(31, 1), u)
            else:
                pe_unit(grp(range(KC)), [("S", s) for s in sO] + kA(24, 8), u)
            if i == NT - 1:
                cs.advance()
            dve(lambda e, u=u, i=i: e.tensor_tensor(out=x1[:, i, :], in0=PU[u][:, 0:D], in1=x1[:, i, :], op=ALU.add),
                reads=[("PU", u)] + kA(2 * i, 2), writes=kA(2 * i, 2))
            if 1 <= i < NT - 1:
                cs.advance()
        cs.drain_keep_last()
        return cs

    def final_tile(hf, i):
        ot = T[3 + (i % 3)]
        tk = ("T", 3 + (i % 3))
        rms_tile(x1[:, i, :], kA(2 * i, 2), P, ot[:, 0:D], [tk], 1)
        r0 = hf * TM + i * P
        plan.op("sp", lambda e: e.dma_start(out=out_d[r0:r0 + P, :], in_=ot[:, 0:D]),
                reads=[tk], writes=[("out", hf, i)], dma=True)

    def phase_D(hf, next_stream, cstream):
        for hh in range(2):
            def load_wd(hh=hh):
                plan.op("pool", lambda e: e.dma_start(
                    out=wdT[:, :, :], in_=w_down_d[hh * FH * P:(hh + 1) * FH * P, :].rearrange("(k p) n -> p k n", p=P)),
                    writes=kA(27, FH), dma=True)
            for cp in range(6):
                nco = SLOTW if cp < 5 else P
                c0 = hh * FH * P + cp * SLOTW
                sG = slot_load(w_gate_d, 0, KC, c0, nco)
                sU = slot_load(w_up_d, 0, KC, c0, nco)
                if cp == 2:
                    load_wd()
                for jj in range(nco // P):
                    c = cp * 2 + jj
                    sg = T[c % 2]
                    first = (hh == 0 and c == 0)
                    if first:
                        hg, hu = proj_begin(sG, 0), proj_begin(sU, 0)
                        cstream.finish()
                        u = proj_end(hg)
                    else:
                        u = proj_unit(sG, jj * P, False)
                    act(lambda e, u=u, sg=sg: e.activation(out=sg[:, 0:TM], in_=PU[u][:, 0:TM], func=AF.Silu),
                        reads=[("PU", u)], writes=[("T", c % 2)])
                    u = proj_end(hu) if first else proj_unit(sU, jj * P, False)
                    dve(lambda e, u=u, sg=sg, c=c: e.tensor_tensor(out=actT[:, c, :], in0=PU[u][:, 0:TM], in1=sg[:, 0:TM],
                                                                   op=ALU.mult),
                        reads=[("PU", u), ("T", c % 2)], writes=kA(16 + c))
            if hh == 1 and next_stream is not None:
                load_gain(0, 0)
                next_stream.advance()
            for i in range(NT):
                u = next_pu()

                def dgrp(kcs):
                    return [(PU[u][:, nb2 * 512:(nb2 + 1) * 512], actT[:, kc, i * P:(i + 1) * P],
                             wdT[:, kc, nb2 * 512:(nb2 + 1) * 512], kc == 0, kc == FH - 1)
                            for nb2 in range(2) for kc in kcs]
                if i == 0:
                    pe_unit(dgrp(range(FH - 1)), kA(16, FH - 1) + kA(27, FH), u)
                    pe_unit(dgrp([FH - 1]), kA(16 + FH - 1, 1) + kA(27, FH), u)
                else:
                    pe_unit(dgrp(range(FH)), kA(16, 2 * FH), u)
                dve(lambda e, u=u, i=i: e.tensor_tensor(out=x1[:, i, :], in0=PU[u][:, 0:D], in1=x1[:, i, :], op=ALU.add),
                    reads=[("PU", u)] + kA(2 * i, 2), writes=kA(2 * i, 2))
                if hh == 1:
                    if i >= 1:
                        final_tile(hf, i - 1)
                    if next_stream is not None:
                        next_stream.advance()
            if hh == 1:
                final_tile(hf, NT - 1)
            if hh == 1 and next_stream is not None:
                next_stream.drain_keep_last()

    load_gain(0, 0)
    a0 = make_A_stream(0, 2, staged=True)
    a0._loads(1)
    for hx in range(NT):
        plan.op("sp", lambda e, hx=hx: e.dma_start(out=x1[:, hx, :], in_=x_d[0, HALO + hx * P:HALO + (hx + 1) * P, :]),
                writes=kA(hx * 2, 2), dma=True)
    phase_K_loads()
    a0.finish()
    phase_K_norm1()
    dump("hT0", AR[:, 0:NH], kH, (P, NH), BF16)
    nxt = None
    for hf in range(2):
        phase_B(hf, nxt)
        cst = phase_B5_C(hf)
        if hf == 0:
            load_gain(1, 3)
        nxt = make_A_stream(1, 3) if hf == 0 else None
        phase_D(hf, nxt, cst)

    plan.op("sp", lambda e: e.nop(), reads=[("out", hf, i) for hf in range(2) for i in range(NT)])

    sems = {}
    for e in Plan.ENG:
        sems[e] = nc.alloc_semaphore("c_" + e)
    for q in ("sp", "pool"):
        for i in range(Plan.ND):
            sems[("dma", q, i)] = nc.alloc_semaphore("d_%s%d" % (q, i))

    def runner(ename):
        def f(eng):
            for waits, fn, semkey, inc in plan.ops[ename]:
                for sk, v in waits:
                    eng.wait_ge(sems[sk], v)
                ins = fn(eng)
                ins.then_inc(sems[semkey], inc)
        return f

    with nc.Block() as block:
        block.tensor(runner("pe"))
        block.scalar(runner("act"))
        block.vector(runner("dve"))
        block.gpsimd(runner("pool"))
        block.sync(runner("sp"))
    return nc, dbg


def _host_inputs(inputs):
    f = lambda a: np.ascontiguousarray(np.asarray(a, dtype=np.float32))
    x = f(inputs["x"])
    mem = f(inputs["mem"])
    B, S, _ = x.shape
    gains = np.stack([np.broadcast_to(f(inputs[k]).reshape(1, D), (P, D))
                      for k in ("norm_mix", "norm_mem", "norm_ffn", "norm_final")]).astype(np.float32)
    conv_w = f(inputs["conv_w"]).reshape(3, 8, P)
    pscale = f(inputs["pool_scale"]).reshape(8, P)
    ident = np.eye(P, dtype=np.float32)
    shared = {
        "gains": np.ascontiguousarray(gains), "ident": ident,
        "w_in": f(inputs["w_in"])[0], "w_conv_out": f(inputs["w_conv_out"])[0], "w_pool": f(inputs["w_pool"])[0],
        "w_kv": f(inputs["w_kv"])[0], "w_xattn_out": f(inputs["w_xattn_out"])[0], "w_out": f(inputs["w_out"])[0],
        "w_gate": f(inputs["w_gate"])[0], "w_up": f(inputs["w_up"])[0], "w_down": f(inputs["w_down"])[0],
    }
    in_maps = []
    for c in range(N_CORES):
        b, half = c // 2, c % 2
        xs = np.zeros((2, TH, D), np.float32)
        small = np.zeros((P, 160), np.float32)
        small[:, 0:24] = conv_w.transpose(2, 1, 0).reshape(P, 24)
        small[:, 24:32] = pscale.T
        for hf in range(2):
            t0 = half * 2 * TM + hf * TM
            lo = t0 - HALO
            if lo >= 0:
                xs[hf] = x[b, lo:t0 + TM]
            else:
                xs[hf, HALO:] = x[b, t0:t0 + TM]
            for g in range(4):
                w = 2 << g
                pos = t0 + np.arange(16) + 1
                cnt = np.minimum(pos, w).astype(np.float32)
                small[:, 32 + hf * 64 + g * 16: 32 + hf * 64 + (g + 1) * 16] = (np.float32(1.0) / cnt)[None, :]
        m = {"x": xs, "mem": np.ascontiguousarray(mem[b]), "small": small}
        m.update(shared)
        in_maps.append(m)
    return in_maps, (B, S)


_CACHE = {}


def kernel(**inputs):
    in_maps, (B, S) = _host_inputs(inputs)
    if "nc" not in _CACHE:
        _CACHE["nc"] = build_program(debug=False)[0]
    nc = _CACHE["nc"]
    res = run_bass_kernel_spmd(nc, in_maps, core_ids=list(range(N_CORES)))
    out = np.empty((B, S, D), np.float32)
    for c in range(N_CORES):
        b, half = c // 2, c % 2
        out[b, half * 2 * TM:(half + 1) * 2 * TM] = np.asarray(res.results[c]["out"], dtype=np.float32)
    return out
```
